# Optimizing a Trainium2 kernel written in Bass

```python
import jax, jax.numpy as jnp
from jax import lax
import numpy as np

D_MODEL = 1024
BATCH = 2
SEQ = 16384
DEPTH = 4

HG_HEADS = 8
HG_DK = 128
HG_DV = 128
HG_WIDTH = HG_HEADS * HG_DK
HG_CHUNK = 64
ATT_Q_HEADS = 16
ATT_KV_HEADS = 4
ATT_GROUP = ATT_Q_HEADS // ATT_KV_HEADS
ATT_HEAD_DIM = 64
ATT_WIDTH = ATT_Q_HEADS * ATT_HEAD_DIM
ATT_KV_WIDTH = ATT_KV_HEADS * ATT_HEAD_DIM
WINDOW = 128
ATT_BLOCK = 128
ROPE_THETA = 500000.0
ROPE_DIM = ATT_HEAD_DIM // 4
FFN_HIDDEN = ((8 * D_MODEL // 3 + 255) // 256) * 256
EPS = 1e-6
MIN_F = 1e-30
IN_SIZES = (HG_WIDTH, HG_WIDTH, HG_WIDTH, HG_WIDTH, ATT_WIDTH, ATT_KV_WIDTH, ATT_KV_WIDTH, D_MODEL, D_MODEL)
IN_COLS = 4 * HG_WIDTH + ATT_WIDTH + 2 * ATT_KV_WIDTH + 2 * D_MODEL

kernel_name = "hgrn2_swa_sink_gated_hybrid"


def rmsnorm(x, g):
    xf = x.astype(jnp.float32)
    y = xf * lax.rsqrt(jnp.mean(xf * xf, axis=-1, keepdims=True) + EPS)
    return (y * g.astype(jnp.float32)).astype(x.dtype)


def split_cols(proj):
    points, acc = [], 0
    for s in IN_SIZES[:-1]:
        acc += s
        points.append(acc)
    return jnp.split(proj, points, axis=-1)


def rope_partial(x, pos):
    half = ROPE_DIM // 2
    inv = ROPE_THETA ** (-jnp.arange(half, dtype=jnp.float32) * 2.0 / ROPE_DIM)
    ang = pos.astype(jnp.float32)[:, None] * inv[None, :]
    cos = jnp.cos(ang)[None, :, None, :]
    sin = jnp.sin(ang)[None, :, None, :]
    xr = x[..., :ROPE_DIM].astype(jnp.float32)
    x1, x2 = xr[..., :half], xr[..., half:]
    rot = jnp.concatenate([x1 * cos - x2 * sin, x2 * cos + x1 * sin], axis=-1)
    return jnp.concatenate([rot.astype(x.dtype), x[..., ROPE_DIM:]], axis=-1)


def hgrn2_chunked(q, k, v, logf):
    B, S, H, DK = q.shape
    DV = v.shape[-1]
    C = HG_CHUNK
    n = S // C

    def to_chunks(t):
        return t.reshape(B, n, C, H, t.shape[-1]).transpose(1, 0, 3, 2, 4)

    qc, kc, vc, fc = to_chunks(q), to_chunks(k), to_chunks(v), to_chunks(logf)
    causal = jnp.tril(jnp.ones((C, C), dtype=bool))

    def step(state, inp):
        qi, ki, vi, fi = inp
        b = jnp.cumsum(fi, axis=2)
        o_inter = jnp.einsum('bhtk,bhkv->bhtv', qi * jnp.exp(b), state)
        diff = b[:, :, :, None, :] - b[:, :, None, :, :]
        decay = jnp.exp(jnp.where(causal[:, :, None], diff, -jnp.inf))
        scores = jnp.einsum('bhtk,bhsk,bhtsk->bhts', qi, ki, decay)
        o_intra = jnp.einsum('bhts,bhsv->bhtv', scores, vi)
        b_last = b[:, :, -1:, :]
        new_state = jnp.exp(b_last[:, :, 0, :])[..., None] * state + jnp.einsum(
            'bhsk,bhsv->bhkv', ki * jnp.exp(b_last - b), vi)
        return new_state, o_inter + o_intra

    init = jnp.zeros((B, H, DK, DV), jnp.float32)
    _, o = lax.scan(step, init, (qc, kc, vc, fc))
    return o.transpose(1, 0, 3, 2, 4).reshape(B, S, H, DV)


def swa_with_sinks(q, k, v, sinks):
    B, S, Hq, hd = q.shape
    W = ATT_BLOCK
    n = S // W
    qb = q.reshape(B, n, W, ATT_KV_HEADS, ATT_GROUP, hd)

    def with_prev(t):
        tb = t.reshape(B, n, W, ATT_KV_HEADS, hd)
        prev = jnp.pad(tb, ((0, 0), (1, 0), (0, 0), (0, 0), (0, 0)))[:, :-1]
        return jnp.concatenate([prev, tb], axis=2)

    kw, vw = with_prev(k), with_prev(v)
    s = jnp.einsum('bnqhgd,bnshd->bnhgqs', qb, kw).astype(jnp.float32) * (hd ** -0.5)
    blk = jnp.arange(n)[:, None, None] * W
    qpos = blk + jnp.arange(W)[None, :, None]
    kpos = blk - W + jnp.arange(2 * W)[None, None, :]
    delta = qpos - kpos
    mask = (delta >= 0) & (delta < WINDOW) & (kpos >= 0)
    s = jnp.where(mask[None, :, None, None], s, -jnp.inf)
    sink = sinks.astype(jnp.float32).reshape(1, 1, ATT_KV_HEADS, ATT_GROUP, 1, 1)
    m = jnp.maximum(jnp.max(s, axis=-1, keepdims=True), sink)
    p = jnp.exp(s - m)
    p = p / (jnp.sum(p, axis=-1, keepdims=True) + jnp.exp(sink - m))
    o = jnp.einsum('bnhgqs,bnshd->bnqhgd', p.astype(v.dtype), vw)
    return o.reshape(B, S, Hq * hd)


def setup_inputs(seed: int = 0) -> dict:
    key = jax.random.key(seed)
    ks = jax.random.split(key, 16)
    f32 = jnp.float32
    nrm = lambda k, shape, scale: jax.random.normal(k, shape, f32) * scale
    return {
        "x": nrm(ks[0], (BATCH, SEQ, D_MODEL), 1.0),
        "norm1": 1.0 + nrm(ks[1], (DEPTH, D_MODEL), 0.01),
        "w_in": nrm(ks[2], (DEPTH, D_MODEL, IN_COLS), D_MODEL ** -0.5),
        "lb_logits": nrm(ks[3], (DEPTH, HG_WIDTH), 0.5),
        "hg_norm": 1.0 + nrm(ks[4], (DEPTH, HG_WIDTH), 0.01),
        "attn_sinks": nrm(ks[5], (DEPTH, ATT_Q_HEADS), 0.5),
        "w_pa": nrm(ks[6], (DEPTH, HG_WIDTH, D_MODEL), HG_WIDTH ** -0.5),
        "w_pb": nrm(ks[7], (DEPTH, ATT_WIDTH, D_MODEL), ATT_WIDTH ** -0.5),
        "w_o": nrm(ks[8], (DEPTH, D_MODEL, D_MODEL), D_MODEL ** -0.5),
        "norm2": 1.0 + nrm(ks[9], (DEPTH, D_MODEL), 0.01),
        "w_gate": nrm(ks[10], (DEPTH, D_MODEL, FFN_HIDDEN), D_MODEL ** -0.5),
        "w_up": nrm(ks[11], (DEPTH, D_MODEL, FFN_HIDDEN), D_MODEL ** -0.5),
        "w_down": nrm(ks[12], (DEPTH, FFN_HIDDEN, D_MODEL), FFN_HIDDEN ** -0.5),
        "final_norm": 1.0 + nrm(ks[13], (D_MODEL,), 0.01),
    }


def reference(x, norm1, w_in, lb_logits, hg_norm, attn_sinks, w_pa, w_pb, w_o,
              norm2, w_gate, w_up, w_down, final_norm):
    B, S, _ = x.shape
    pos = jnp.arange(S)
    lb_p = jax.nn.softmax(lb_logits.astype(jnp.float32), axis=0)
    lb_all = jnp.cumsum(lb_p, axis=0) - lb_p[0:1]

    for l in range(DEPTH):
        h = rmsnorm(x, norm1[l])
        proj = h @ w_in[l]
        hq, hf, hi, hg, aq, ak, av, ga, gb = split_cols(proj)

        q = jax.nn.silu(hq).reshape(B, S, HG_HEADS, HG_DK).astype(jnp.float32)
        z = hf.reshape(B, S, HG_HEADS, HG_DK).astype(jnp.float32)
        lb = lb_all[l].reshape(HG_HEADS, HG_DK)
        f = lb + (1.0 - lb) * jax.nn.sigmoid(z)
        logf = jnp.log(jnp.maximum(f, MIN_F))
        kk = 1.0 - f
        vi = hi.reshape(B, S, HG_HEADS, HG_DV).astype(jnp.float32)
        o_hg = hgrn2_chunked(q, kk, vi, logf)
        o_hg = rmsnorm(o_hg, hg_norm[l].reshape(HG_HEADS, HG_DV)).astype(x.dtype)
        o_hg = o_hg.reshape(B, S, HG_WIDTH) * jax.nn.silu(hg)
        y_a = o_hg @ w_pa[l]

        qa = rope_partial(aq.reshape(B, S, ATT_Q_HEADS, ATT_HEAD_DIM), pos)
        ka = rope_partial(ak.reshape(B, S, ATT_KV_HEADS, ATT_HEAD_DIM), pos)
        va = av.reshape(B, S, ATT_KV_HEADS, ATT_HEAD_DIM)
        y_b = swa_with_sinks(qa, ka, va, attn_sinks[l]) @ w_pb[l]

        mix = jax.nn.sigmoid(ga) * y_a + jax.nn.sigmoid(gb) * y_b
        x = x + mix @ w_o[l]

        h2 = rmsnorm(x, norm2[l])
        x = x + (jax.nn.silu(h2 @ w_gate[l]) * (h2 @ w_up[l])) @ w_down[l]

    return rmsnorm(x, final_norm)
```

```python
from contextlib import ExitStack

import numpy as np
import ml_dtypes

import concourse.bass as bass
import concourse.mybir as mybir
from concourse.bass_utils import run_bass_kernel_spmd

F32 = mybir.dt.float32
BF16 = mybir.dt.bfloat16
AF = mybir.ActivationFunctionType
ALU = mybir.AluOpType
AX = mybir.AxisListType

D = 1024
NH = 8
DK = 128
FF = 2816
NJ = FF // 128
INC = 7680
EPS = 1e-6
NCORES = 8
SEG = 4


class Op:
    __slots__ = ("id", "eng", "fn", "deps", "is_dma", "semkey", "ndma", "sig", "has_dep", "epoch")


class Prog:
    def __init__(self):
        self.ops = []
        self.last_w = {}
        self.readers = {}
        self.epoch = 0
        self.last_dma_on_sem = {}

    def alias(self, new_key, old_keys):
        rs = list(self.readers.get(new_key, []))
        for k in old_keys:
            w = self.last_w.get(k)
            if w is not None:
                rs.append(w)
            rs.extend(self.readers.get(k, []))
        self.readers[new_key] = rs

    def add(self, eng, fn, reads=(), writes=(), semkey=None, ndma=1):
        op = Op()
        op.id = len(self.ops)
        op.eng = eng
        op.fn = fn
        op.is_dma = semkey is not None
        op.semkey = semkey
        op.ndma = ndma
        op.sig = None
        op.has_dep = False
        op.epoch = self.epoch
        deps = {}
        for b in reads:
            w = self.last_w.get(b)
            if w is not None:
                deps.setdefault(w, set()).add("RAW")
        for b in writes:
            w = self.last_w.get(b)
            if w is not None:
                deps.setdefault(w, set()).add("WAW")
            for r in self.readers.get(b, ()):
                deps.setdefault(r, set()).add("WAR")
        keep = []
        for d, kinds in deps.items():
            dop = self.ops[d]
            if dop.is_dma:
                keep.append(d)
            elif dop.eng == eng:
                if eng == "pe":
                    continue
                if kinds & {"RAW", "WAW"}:
                    keep.append(d)
            else:
                keep.append(d)
        op.deps = sorted(keep)
        if op.is_dma:
            prev = self.last_dma_on_sem.get(semkey)
            if prev is not None:
                seen = set()
                frontier = list(op.deps)
                ok = False
                for _ in range(4):
                    nxt = []
                    for d in frontier:
                        if d == prev:
                            ok = True
                            break
                        if d in seen or d < prev:
                            continue
                        seen.add(d)
                        nxt.extend(self.ops[d].deps)
                    if ok:
                        break
                    frontier = nxt
                assert ok, f"DMA sem {semkey} reused while previous DMA may be in flight (op {op.id})"
            self.last_dma_on_sem[semkey] = op.id
        for d in op.deps:
            self.ops[d].has_dep = True
        for b in writes:
            self.last_w[b] = op.id
            self.readers[b] = []
        for b in reads:
            self.readers.setdefault(b, []).append(op.id)
        self.ops.append(op)
        return op

    def emit(self, nc, es, final_wait_ops=()):
        engs = ["pe", "act", "dve", "pool", "sp"]
        nep = self.epoch + 1
        esem = {}
        for e in engs:
            for ep in range(nep):
                if any(o.eng == e and o.epoch == ep and not o.is_dma for o in self.ops):
                    esem[(e, ep)] = es.enter_context(nc.semaphore(f"s_{e}_{ep}"))
        dsem = {}
        for o in self.ops:
            if o.is_dma and o.semkey not in dsem:
                dsem[o.semkey] = es.enter_context(nc.semaphore(f"d_{o.semkey}"))
        ecount = {}
        dcount = {}
        fw = set(final_wait_ops)
        for o in self.ops:
            if o.is_dma:
                c = dcount.get(o.semkey, 0) + 16 * o.ndma
                dcount[o.semkey] = c
                o.sig = (dsem[o.semkey], c)
            elif o.has_dep or o.id in fw:
                k = (o.eng, o.epoch)
                c = ecount.get(k, 0) + 1
                ecount[k] = c
                o.sig = (esem[k], c)
        self.maxcount = max(list(ecount.values()) + list(dcount.values()) + [0])
        per_eng = {e: [o for o in self.ops if o.eng == e] for e in engs}
        ops = self.ops
        nwaits = [0]

        def run_engine(engobj, lst, final=False):
            known = {}
            for o in lst:
                for d in o.deps:
                    sem, val = ops[d].sig
                    key = id(sem)
                    if known.get(key, 0) >= val:
                        continue
                    engobj.wait_ge(sem, val)
                    nwaits[0] += 1
                    known[key] = val
                if o.is_dma:
                    o.fn(engobj, o.sig[0])
                else:
                    ins = o.fn(engobj)
                    if o.sig is not None:
                        ins.then_inc(o.sig[0], 1)
            if final:
                for oid in final_wait_ops:
                    sem, val = ops[oid].sig
                    if known.get(id(sem), 0) >= val:
                        continue
                    engobj.wait_ge(sem, val)
                    known[id(sem)] = val

        with nc.Block() as block:
            @block.tensor
            def _(e):
                run_engine(e, per_eng["pe"])

            @block.scalar
            def _(e):
                run_engine(e, per_eng["act"])

            @block.vector
            def _(e):
                run_engine(e, per_eng["dve"])

            @block.gpsimd
            def _(e):
                run_engine(e, per_eng["pool"])

            @block.sync
            def _(e):
                run_engine(e, per_eng["sp"], final=True)
        self.nwaits = nwaits[0]


class Region:
    def __init__(self, prog, tensor, nbytes, name):
        self.prog = prog
        self.t = tensor
        self.nbytes = nbytes
        self.name = name
        self.cur = []
        self.hist = []
        self.off = 0

    def reset(self):
        self.hist = (self.hist + self.cur)[-600:]
        self.cur = []
        self.off = 0

    def truncate(self, n_keep):
        self.hist = (self.hist + self.cur[n_keep:])[-600:]
        self.cur = self.cur[:n_keep]
        self.off = self.cur[-1][1] if self.cur else 0

    def alloc(self, key, shape_free, dtype, parts=128):
        esz = 4 if dtype == F32 else 2
        n = 1
        for s in shape_free:
            n *= s
        nb = n * esz
        start = (self.off + 3) // 4 * 4
        end = start + nb
        assert end <= self.nbytes, f"region {self.name} overflow: {end} > {self.nbytes} for {key}"
        self.off = end
        self.cur.append((start, end, key))
        ap = self.t[0:parts, start // 2: end // 2]
        if dtype == F32:
            ap = ap.bitcast(F32)
        if len(shape_free) == 2:
            ap = ap.rearrange("p (a b) -> p a b", a=shape_free[0])
        elif len(shape_free) == 3:
            ap = ap.rearrange("p (a b c) -> p a b c", a=shape_free[0], b=shape_free[1])
        return ap

    def alias_keys(self, keys_of):
        for (s, e, k) in self.cur:
            olds = []
            for (hs, he, hk) in self.hist:
                if hs < e and he > s:
                    olds.extend(keys_of(hk))
            news = keys_of(k)
            if olds:
                for nk in news:
                    self.prog.alias(nk, [o for o in olds if o != nk])


NEG = -30000.0
AGC = 1024 + 8 + 512 + 256


class Builder:
    def __init__(self, T, mode="p2", final=False, debug=False, stop_after=None):
        self.T = T
        self.L = 1
        self.mode = mode
        self.final = final
        self.stop_after = stop_after
        self.out_ops = []
        self.NT = T // 512
        self.NB = T // 128
        self.nc = bass.Bass("TRN2", target_bir_lowering=False)
        self.P = Prog()
        self.subkeys = {}
        self.debug = debug
        self.dbg_ops = []
        self.dbg_names = []

    def reg(self, key, subs=None):
        self.subkeys[key] = list(subs) if subs is not None else [key]

    def keys_of(self, key):
        return self.subkeys.get(key, [key])

    def mm(self, out, lhsT, rhs, start, stop, reads, writes):
        self.P.add("pe", lambda e: e.matmul(out=out, lhsT=lhsT, rhs=rhs, start=start, stop=stop), reads, writes)

    def tr(self, out, in_, reads, writes):
        ident = self.ident
        self.P.add("pe", lambda e: e.transpose(out=out, in_=in_, identity=ident), list(reads) + ["ident"], writes)

    def act(self, out, in_, func, reads, writes, scale=1.0, accum_out=None):
        if accum_out is None:
            fn = lambda e: e.activation(out=out, in_=in_, func=func, scale=scale)
        else:
            fn = lambda e: e.activation(out=out, in_=in_, func=func, scale=scale, accum_out=accum_out)
        self.P.add("act", fn, reads, writes)

    def tt(self, out, in0, in1, op, reads, writes, eng="dve"):
        self.P.add(eng, lambda e: e.tensor_tensor(out=out, in0=in0, in1=in1, op=op), reads, writes)

    def ts(self, out, in0, s1, s2, op0, op1, reads, writes, eng="dve"):
        self.P.add(eng, lambda e: e.tensor_scalar(out=out, in0=in0, scalar1=s1, scalar2=s2, op0=op0, op1=op1),
                   reads, writes)

    def stt(self, out, in0, scalar, in1, op0, op1, reads, writes):
        self.P.add("dve", lambda e: e.scalar_tensor_tensor(out=out, in0=in0, scalar=scalar, in1=in1, op0=op0, op1=op1),
                   reads, writes)

    def cpy(self, out, in_, reads, writes, eng="dve"):
        self.P.add(eng, lambda e: e.tensor_copy(out=out, in_=in_), reads, writes)

    def memset(self, ap, val, writes, eng="dve"):
        self.P.add(eng, lambda e: e.memset(ap, val), [], writes)

    def recip(self, out, in_, reads, writes):
        self.P.add("dve", lambda e: e.reciprocal(out=out, in_=in_), reads, writes)

    def scan(self, out, d0, d1, initial, reads, writes):
        self.P.add("dve", lambda e: e.tensor_tensor_scan(out=out, data0=d0, data1=d1, initial=initial,
                                                         op0=ALU.mult, op1=ALU.add), reads, writes)

    def dma(self, eng, out, in_, reads, writes, semkey, **kw):
        return self.P.add(eng, lambda e, s: e.dma_start(out=out, in_=in_, **kw).then_inc(s, 16), reads, writes,
                          semkey=semkey)

    def dump(self, name, ap, reads):
        if not self.debug:
            return
        d = self.nc.dram_tensor("dbg_" + name, list(ap.shape), ap.dtype, kind="ExternalOutput").ap()
        op = self.dma("sp", d, ap, list(reads), [("dbg", name)], "dbg_" + name)
        self.dbg_ops.append(op.id)
        self.dbg_names.append("dbg_" + name)

    def dram_in(self, name, shape, dtype=F32):
        return self.nc.dram_tensor(name, list(shape), dtype, kind="ExternalInput").ap()

    def dram_scratch(self, name, shape, dtype=F32):
        if self.debug:
            self.dbg_names.append(name)
            return self.nc.dram_tensor(name, list(shape), dtype, kind="ExternalOutput").ap()
        return self.nc.dram_tensor(name, list(shape), dtype).ap()

    def load_bcast(self, dst_ap, src_row_ap, key, semkey):
        n = src_row_ap.shape[-1]
        self.dma("sp", dst_ap, src_row_ap.to_broadcast([128, n]), [], [key], semkey)

    def load_w_cast(self, dst_ap, src_ap, c0, ncols, key, semkey, nk):
        srcv = src_ap.rearrange("(kc p) n -> p kc n", p=128)
        pieces = []
        for a0 in range(0, ncols, 512):
            a1 = min(ncols, a0 + 512)
            for k0 in range(0, nk, 4):
                k1 = min(nk, k0 + 4)
                pieces.append((k0, k1, a0, a1))

        def fn(e, s):
            for (k0, k1, a0, a1) in pieces:
                e.dma_start(out=dst_ap[:, k0:k1, a0:a1], in_=srcv[:, k0:k1, c0 + a0:c0 + a1]).then_inc(s, 16)
        self.P.add("pool", fn, [], [key], semkey=semkey, ndma=len(pieces))

    def build(self):
        nc = self.nc
        T, L = self.T, self.L
        P = self.P
        NB = self.NB
        mode = self.mode
        p1, p2 = mode == "p1", mode == "p2"
        self.x_in = self.dram_in("x", [T, D])
        self.norm1 = self.dram_in("norm1", [L, D])
        self.w_in = self.dram_in("w_in", [L, D, INC])
        self.cbf_d = self.dram_in("cbf", [128, 128 + 64 + 512], BF16)
        self.sel_d = self.dram_in("sel", [128, 8])
        self.cos_d = self.dram_in("rope_cos", [128, T])
        self.sin_d = self.dram_in("rope_sin", [128, T])
        if p1:
            self.lb_logits = self.dram_in("lb_logits", [4, D])
            self.lsel_d = self.dram_in("lsel", [128, 4])
            self.ohl = nc.dram_tensor("ohl", [NH, 128, T], F32, kind="ExternalOutput").ap()
            self.qgd = nc.dram_tensor("qgd", [NH, 128, T], BF16, kind="ExternalOutput").ap()
            self.agin = nc.dram_tensor("agin", [128, AGC], F32, kind="ExternalOutput").ap()
            self.y = None
        if p2:
            self.hg_norm = self.dram_in("hg_norm", [L, D])
            self.attn_sinks = self.dram_in("attn_sinks", [L, 16])
            self.w_pa = self.dram_in("w_pa", [L, D, D])
            self.w_pb = self.dram_in("w_pb", [L, D, D])
            self.w_o = self.dram_in("w_o", [L, D, D])
            self.norm2 = self.dram_in("norm2", [L, D])
            self.w_gate = self.dram_in("w_gate", [L, D, FF])
            self.w_up = self.dram_in("w_up", [L, D, FF])
            self.w_down = self.dram_in("w_down", [L, FF, D])
            self.final_norm = self.dram_in("final_norm", [1, D])
            self.ohl = self.dram_in("ohl", [NH, 128, T])
            self.qgd = self.dram_in("qgd", [NH, 128, T], BF16)
            self.agout = self.dram_in("agout", [128 * SEG, AGC])
            self.agin = None
            self.y = nc.dram_tensor("y", [T, D], F32, kind="ExternalOutput").ap()
            self.xr0 = self.dram_scratch("xr0", [T, D])
            self.xr1 = self.dram_scratch("xr1", [T, D])
            self.attTd = self.dram_scratch("attTd", [8, 128, T], BF16)
            self.ohgd = self.dram_scratch("ohgd", [NH, 128, T], BF16)

        with ExitStack() as es:
            RWB, RHB, RMB, RSB = 136192, 65536, 8192, 2048
            rw_t = es.enter_context(nc.sbuf_tensor("RW", [128, RWB // 2], BF16))
            rh_t = es.enter_context(nc.sbuf_tensor("RH", [128, RHB // 2], BF16))
            rm_t = es.enter_context(nc.sbuf_tensor("RM", [128, RMB // 2], BF16))
            rs_t = es.enter_context(nc.sbuf_tensor("RS", [128, RSB // 2], BF16))
            self.RW = Region(P, rw_t, RWB, "RW")
            self.RH = Region(P, rh_t, RHB, "RH")
            RM = Region(P, rm_t, RMB, "RM")
            RS = Region(P, rs_t, RSB, "RS")
            self.ps = [es.enter_context(nc.psum_tensor(f"ps{b}", [128, 512], F32))[:, :] for b in range(8)]
            self.g = RM.alloc("g", [D], F32)
            cbf = RM.alloc("cbf", [128 + 64 + 512], BF16)
            self.ident = cbf[:, 0:128]
            self.maskneg = cbf[:, 128:192]
            self.maskb = cbf[:, 192:704]
            self.maskb0 = RM.alloc("maskb0", [512], BF16)
            self.ones = RM.alloc("ones", [128], BF16)
            self.stat = RM.alloc("stat", [8, 4], F32)
            self.lball = RM.alloc("lball", [4, 8], F32)
            self.sel = RM.alloc("sel", [8], F32)
            self.esink = RM.alloc("esink", [16], F32)
            self.lbt = RM.alloc("lbt", [6, 4, 8], F32)
            self.ones8 = RM.alloc("ones8", [8], F32)
            self.Sin_bf = RS.alloc("Sin_bf", [NH, 128], BF16)
            self.dma("sp", cbf, self.cbf_d, [], ["ident", "maskneg", "maskb"], "cbf")
            self.dma("sp", self.sel, self.sel_d, [], ["sel"], "sel")
            self.memset(self.ones, 1.0, ["ones"])
            self.memset(self.ones8, 1.0, ["ones8"])
            self.cpy(self.maskb0, self.maskb, ["maskb"], ["maskb0"])
            mb0v = self.maskb0.rearrange("p (a b) -> p a b", a=4)
            for a in (1, 3):
                self.ts(mb0v[:, a, :], mb0v[:, a, :], self.sel[:, 7:8], None, ALU.add, ALU.bypass,
                        ["maskb0", "sel"], ["maskb0"])
            self.lbc = RM.alloc("lbc", [8], F32)
            self.omlc = RM.alloc("omlc", [8], F32)
            l = 0
            if p1:
                self.lsel = RM.alloc("lsel", [4], F32)
                self.dma("sp", self.lsel, self.lsel_d, [], ["lsel"], "lsel")
                self.lb_prologue()
                self.phase_a(l, self.x_in)
                self.phase_b(l)
                self.phase_c0(l, write_ag=True)
                finals = list(self.out_ops)
            else:
                self.phase_a(l, self.x_in)
                self.phase_c0(l, write_ag=False)
                self.phase_ag(l)
                self.phase_c1(l)
                self.phase_d1(l)
                self.phase_d2(l, self.x_in, self.xr1)
                if self.final:
                    self.ffn_phase(l, self.xr1, self.xr0)
                    finals = self.final_phase(self.xr0)
                else:
                    self.ffn_phase(l, self.xr1, self.y)
                    finals = list(self.out_ops)
            P.emit(nc, es, final_wait_ops=finals + self.dbg_ops)
        return nc

    def lb_prologue(self):
        lg, e, p = self.lbt[:, 0], self.lbt[:, 1], self.lbt[:, 2]
        mx, s, r = self.lbt[:, 3, 0], self.lbt[:, 3, 1], self.lbt[:, 3, 2]
        m2 = self.lbt[:, 3, 3]
        lbl = self.lb_logits

        def fn_lb(e_, sm):
            for l_ in range(4):
                for h_ in range(NH):
                    e_.dma_start(out=lg[:, l_, h_:h_ + 1],
                                 in_=lbl[l_:l_ + 1, h_ * 128:(h_ + 1) * 128].rearrange("o k -> k o")).then_inc(sm, 16)
        self.P.add("sp", fn_lb, [], ["lb_lg"], semkey="lblg", ndma=32)
        self.tt(mx, lg[:, 0], lg[:, 1], ALU.max, ["lb_lg"], ["lb_mx"])
        self.tt(m2, lg[:, 2], lg[:, 3], ALU.max, ["lb_lg"], ["lb_m2"])
        self.tt(mx, mx, m2, ALU.max, ["lb_mx", "lb_m2"], ["lb_mx"])
        self.tt(e, lg, mx.unsqueeze(1).to_broadcast([128, 4, 8]), ALU.subtract, ["lb_lg", "lb_mx"], ["lb_e"])
        self.act(e, e, AF.Exp, ["lb_e"], ["lb_e"])
        self.tt(s, e[:, 0], e[:, 1], ALU.add, ["lb_e"], ["lb_s"])
        self.tt(s, s, e[:, 2], ALU.add, ["lb_e", "lb_s"], ["lb_s"])
        self.tt(s, s, e[:, 3], ALU.add, ["lb_e", "lb_s"], ["lb_s"])
        self.recip(r, s, ["lb_s"], ["lb_r"])
        self.tt(p, e, r.unsqueeze(1).to_broadcast([128, 4, 8]), ALU.mult, ["lb_e", "lb_r"], ["lb_p"])
        lb = self.lball
        self.memset(lb[:, 0], 0.0, ["lball"])
        self.cpy(lb[:, 1], p[:, 1], ["lb_p", "lball"], ["lball"])
        self.tt(lb[:, 2], lb[:, 1], p[:, 2], ALU.add, ["lb_p", "lball"], ["lball"])
        self.tt(lb[:, 3], lb[:, 2], p[:, 3], ALU.add, ["lb_p", "lball"], ["lball"])
        self.ts(self.lbc, lb[:, 0], self.lsel[:, 0:1], None, ALU.mult, ALU.bypass, ["lball", "lsel"], ["lbc"])
        for l_ in range(1, 4):
            self.stt(self.lbc, lb[:, l_], self.lsel[:, l_:l_ + 1], self.lbc, ALU.mult, ALU.add,
                     ["lball", "lsel", "lbc"], ["lbc"])
        self.ts(self.omlc, self.lbc, -1.0, 1.0, ALU.mult, ALU.add, ["lbc"], ["omlc"])

    def rms_block(self, xb, xkey, slot, out_ap, out_key, g_ap, g_key, junk_ap, junk_key):
        st = self.stat
        kss, k1, k2, k3 = [("stat", slot, c) for c in range(4)]
        self.act(junk_ap, xb, AF.Square, [xkey], [junk_key, kss], accum_out=st[:, slot, 0:1])
        self.ts(st[:, slot, 1:2], st[:, slot, 0:1], 1.0 / D, EPS, ALU.mult, ALU.add, [kss], [k1])
        self.act(st[:, slot, 2:3], st[:, slot, 1:2], AF.Sqrt, [k1], [k2])
        self.recip(st[:, slot, 3:4], st[:, slot, 2:3], [k2], [k3])
        self.stt(out_ap, xb, st[:, slot, 3:4], g_ap, ALU.mult, ALU.mult, [xkey, k3, g_key], [out_key])

    def mixer(self, l, xin, xout):
        sa = self.stop_after
        self.phase_a(l, xin)
        if sa == "A":
            return True
        self.phase_b(l)
        if sa == "B":
            return True
        self.phase_c0(l)
        self.phase_ag(l)
        if sa == "AG":
            return True
        self.phase_c1(l)
        if sa == "C":
            return True
        self.phase_d1(l)
        if sa == "D1":
            return True
        self.phase_d2(l, xin, xout)
        return False

    def phase_a(self, l, xin):
        P, RW, RH = self.P, self.RW, self.RH
        T, NB = self.T, self.NB
        RW.reset()
        RH.reset()
        self.hT = RH.alloc("hT", [8, T], BF16)
        self.reg("hT", [("hT", b) for b in range(NB)])
        xblk = [RW.alloc(("xblk", s), [D], F32) for s in range(2)]
        hn = [RW.alloc(("hn", s), [D], BF16) for s in range(2)]
        junk = RW.alloc("junk", [D], BF16)
        RW.alias_keys(self.keys_of)
        RH.alias_keys(self.keys_of)
        self.load_bcast(self.g, self.norm1[l:l + 1, :], "g", "g")
        for b in range(NB):
            s = b % 2
            r0 = b * 128
            self.dma("sp", xblk[s], xin[r0:r0 + 128, :], [("dram", xin.tensor.name, r0)], [("xblk", s)], f"xblk{s}")
            self.rms_block(xblk[s], ("xblk", s), s, hn[s], ("hn", s), self.g, "g", junk, "junk")
            pstv = self.ps[s].bitcast(BF16).rearrange("p (k t) -> p k t", k=8)
            for kc in range(8):
                self.tr(pstv[:, kc, :], hn[s][:, kc * 128:(kc + 1) * 128], [("hn", s)], [("ps", s)])
            self.act(self.hT[:, :, r0:r0 + 128], pstv, AF.Copy, [("ps", s)], [("hT", b)])

    def phase_b(self, l):
        P, RW = self.P, self.RW
        T, NT = self.T, self.NT
        hT = self.hT
        ps = self.ps
        RW.reset()
        wq = [RW.alloc(("bwq", s), [8, 128], BF16) for s in range(2)]
        wf = [RW.alloc(("bwf", s), [8, 128], BF16) for s in range(2)]
        wi = [RW.alloc(("bwi", s), [8, 128], BF16) for s in range(2)]
        cmask = RW.alloc("cmask", [512], F32)
        f32n = ["qT", "sg", "f", "lf", "bT", "eb", "enb", "ol"]
        fb = {n: [RW.alloc((n, s), [512], F32) for s in range(2)] for n in f32n}
        bfn = ["qb", "kbn", "kdn", "qg"]
        bb = {n: [RW.alloc((n, s), [512], BF16) for s in range(2)] for n in bfn}
        kdt = [RW.alloc(("kdt", s), [4, 128], BF16) for s in range(2)]
        Vt = [RW.alloc(("Vt", s), [4, 128], BF16) for s in range(2)]
        AT = [RW.alloc(("AT", s), [4, 64], BF16) for s in range(2)]
        S32 = [RW.alloc(("S32", s), [128], F32) for s in range(2)]
        Sbf = [RW.alloc(("Sbf", s), [8, 128], BF16) for s in range(2)]
        for s in range(2):
            self.reg(("Sbf", s), [("Sbf", s, c) for c in range(8)])
        incl = [RW.alloc(("incl", s), [8], F32) for s in range(2)]
        bprev = [RW.alloc(("bprev", s), [8], F32) for s in range(2)]
        Eb = [RW.alloc(("Eb", s), [8], F32) for s in range(2)]
        self.agS = RW.alloc("agS", [NH, 128], F32)
        self.agD = RW.alloc("agD", [NH], F32)
        self.reg("agS", [("agS", h) for h in range(NH)])
        self.reg("agD", [("agD", h) for h in range(NH)])
        RW.alias_keys(self.keys_of)

        self.memset(cmask, 1.0, ["cmask"])
        self.memset(cmask.rearrange("p (c t) -> p c t", t=64)[:, :, 0:1], 0.0, ["cmask"])
        psQ, psZ, psV, psKD, psPe, psPo, psA, psO = ps
        psKDv = psKD.bitcast(BF16)[:, 0:512].rearrange("p (a b) -> p a b", a=4)
        it = 0
        for h in range(NH):
            ws = h % 2
            self.load_w_cast(wq[ws], self.w_in[l], h * 128, 128, ("bwq", ws), f"bwq{ws}", 8)
            self.load_w_cast(wf[ws], self.w_in[l], 1024 + h * 128, 128, ("bwf", ws), f"bwf{ws}", 8)
            self.load_w_cast(wi[ws], self.w_in[l], 2048 + h * 128, 128, ("bwi", ws), f"bwi{ws}", 8)
            s32 = S32[ws]
            ks32 = ("S32", ws)
            for i in range(NT):
                p_ = it % 2
                pn = (it + 1) % 2
                it += 1
                t0 = i * 512
                hk = [("hT", i * 4 + j) for j in range(4)]
                qT, sg, f, lf, bT, eb, enb, ol = [fb[n][p_] for n in f32n]
                qb, kbn, kdn, qg = [bb[n][p_] for n in bfn]
                K = lambda n: (n, p_)
                if i == 0:
                    self.memset(s32, 0.0, [ks32])
                    self.memset(Sbf[p_][:, 0, :], 0.0, [("Sbf", p_, 0)])
                for kc in range(8):
                    self.mm(psQ, wq[ws][:, kc, :], hT[:, kc, t0:t0 + 512], kc == 0, kc == 7,
                            hk + [("bwq", ws)], [("ps", 0)])
                for kc in range(8):
                    self.mm(psZ, wf[ws][:, kc, :], hT[:, kc, t0:t0 + 512], kc == 0, kc == 7,
                            hk + [("bwf", ws)], [("ps", 1)])
                for j in range(4):
                    for kc in range(8):
                        self.mm(psV[:, j * 128:(j + 1) * 128], hT[:, kc, t0 + j * 128:t0 + (j + 1) * 128],
                                wi[ws][:, kc, :], kc == 0, kc == 7, [("hT", i * 4 + j), ("bwi", ws)], [("ps", 2)])
                self.act(qT, psQ, AF.Silu, [("ps", 0)], [K("qT")])
                self.act(sg, psZ, AF.Sigmoid, [("ps", 1)], [K("sg")])
                self.act(Vt[p_], psV.rearrange("p (a b) -> p a b", a=4), AF.Copy, [("ps", 2)], [K("Vt")])
                self.ts(f, sg, self.omlc[:, h:h + 1], self.lbc[:, h:h + 1], ALU.mult, ALU.add,
                        [K("sg"), "omlc", "lbc"], [K("f")])
                self.act(lf, f, AF.Ln, [K("f")], [K("lf")])
                self.scan(bT, cmask, lf, 0.0, ["cmask", K("lf")], [K("bT")])
                self.act(eb, bT, AF.Exp, [K("bT")], [K("eb")])
                self.act(enb, bT, AF.Exp, [K("bT")], [K("enb")], scale=-1.0)
                self.tt(qb, qT, eb, ALU.mult, [K("qT"), K("eb")], [K("qb")])
                self.stt(kbn, f, 1.0, enb, ALU.subtract, ALU.mult, [K("f"), K("enb")], [K("kbn")])
                eb3 = eb.rearrange("p (c t) -> p c t", t=64)
                bT3 = bT.rearrange("p (c t) -> p c t", t=64)
                self.tt(kdn.rearrange("p (c t) -> p c t", t=64), kbn.rearrange("p (c t) -> p c t", t=64),
                        eb3[:, :, 63:64].to_broadcast([128, 8, 64]), ALU.mult, [K("kbn"), K("eb")], [K("kdn")])
                init = 0.0 if i == 0 else incl[pn][:, 7:8]
                rinit = [] if i == 0 else [("incl", pn)]
                self.scan(incl[p_], self.ones8, bT3[:, :, 63], init, ["ones8", K("bT")] + rinit, [K("incl")])
                self.tt(bprev[p_], incl[p_], bT3[:, :, 63], ALU.subtract, [K("incl"), K("bT")], [K("bprev")])
                self.act(Eb[p_], bprev[p_], AF.Exp, [K("bprev")], [K("Eb")])
                self.tt(qg.rearrange("p (c t) -> p c t", t=64), qb.rearrange("p (c t) -> p c t", t=64),
                        Eb[p_].unsqueeze(2).to_broadcast([128, 8, 64]), ALU.mult, [K("qb"), K("Eb")], [K("qg")])
                self.out_ops.append(self.dma("sp", self.qgd[h, :, t0:t0 + 512], qg, [K("qg")], [("qgd", h, i)], f"qg{p_}").id)
                for j in range(4):
                    self.tr(psKDv[:, j, :], kdn[:, j * 128:(j + 1) * 128], [K("kdn")], [("ps", 3)])
                self.act(kdt[p_], psKDv, AF.Copy, [("ps", 3)], [K("kdt")])
                for c in range(8):
                    j, p0 = c // 2, (c % 2) * 64
                    pp = psPe if c % 2 == 0 else psPo
                    self.mm(pp[:, j * 128:(j + 1) * 128], kdt[p_][p0:p0 + 64, j, :], Vt[p_][p0:p0 + 64, j, :],
                            True, True, [K("kdt"), K("Vt")], [("ps", 4 + c % 2)])
                for c in range(8):
                    j, p0 = c // 2, (c % 2) * 64
                    self.mm(psA[p0:p0 + 64, j * 64:(j + 1) * 64], kbn[:, c * 64:(c + 1) * 64],
                            qb[:, c * 64:(c + 1) * 64], True, True, [K("kbn"), K("qb")], [("ps", 6)])
                self.tt(AT[p_], psA[:, 0:256].rearrange("p (a b) -> p a b", a=4),
                        self.maskneg.unsqueeze(1).to_broadcast([128, 4, 64]), ALU.mult,
                        [("ps", 6), "maskneg"], [K("AT")])
                for c in range(8):
                    pp = psPe if c % 2 == 0 else psPo
                    self.stt(s32, s32, eb[:, c * 64 + 63:c * 64 + 64], pp[:, (c // 2) * 128:(c // 2 + 1) * 128],
                             ALU.mult, ALU.subtract, [ks32, K("eb"), ("ps", 4 + c % 2)], [ks32])
                    if c < 7:
                        self.act(Sbf[p_][:, c + 1, :], s32, AF.Copy, [ks32], [("Sbf", p_, c + 1)])
                    elif i + 1 < NT:
                        self.act(Sbf[pn][:, 0, :], s32, AF.Copy, [ks32], [("Sbf", pn, 0)])
                for c in range(8):
                    j, p0 = c // 2, (c % 2) * 64
                    self.mm(psO[:, c * 64:(c + 1) * 64], Vt[p_][p0:p0 + 64, j, :], AT[p_][p0:p0 + 64, j, :],
                            True, False, [K("Vt"), K("AT")], [("ps", 7)])
                    self.mm(psO[:, c * 64:(c + 1) * 64], Sbf[p_][:, c, :], qb[:, c * 64:(c + 1) * 64],
                            False, True, [("Sbf", p_, c), K("qb")], [("ps", 7)])
                self.act(ol, psO, AF.Copy, [("ps", 7)], [K("ol")])
                self.out_ops.append(self.dma("sp", self.ohl[h, :, t0:t0 + 512], ol, [K("ol")], [("ohl", h, i)], f"ol{p_}").id)
                if i == NT - 1:
                    self.cpy(self.agS[:, h, :], s32, [ks32], [("agS", h)])
                    self.act(self.agD[:, h:h + 1], incl[p_][:, 7:8], AF.Exp, [K("incl")], [("agD", h)])
        self.dump("agS", self.agS, [("agS", h) for h in range(NH)])
        self.dump("agD", self.agD, [("agD", h) for h in range(NH)])
        self.out_ops.append(self.dma("sp", self.agin[:, 0:1024], self.agS.rearrange("p a b -> p (a b)"),
                                     [("agS", h) for h in range(NH)], ["agin_S"], "aginS").id)
        self.out_ops.append(self.dma("sp", self.agin[:, 1024:1032], self.agD, [("agD", h) for h in range(NH)],
                                     ["agin_D"], "aginD").id)

    def phase_c0(self, l, write_ag=True):
        P, RW = self.P, self.RW
        T, NT, NB = self.T, self.NT, self.NB
        hT, ps = self.hT, self.ps
        RW.reset()
        self.kf2 = RW.alloc("kf2", [4, T + 128], BF16)
        self.reg("kf2", [("kf2", g, b) for g in range(4) for b in range(NB + 1)])
        self.Vall = RW.alloc("Vall", [NB + 1, 4, 65], BF16)
        self.reg("Vall", [("Vall", b) for b in range(NB + 1)])
        self.cosT = RW.alloc("cosT", [T], F32)
        self.sinT = RW.alloc("sinT", [T], F32)
        self.c_mark = len(RW.cur)
        wk2 = [RW.alloc(("wk2", s), [8, 128], BF16) for s in range(2)]
        wkr2 = [RW.alloc(("wkr2", s), [8, 128], BF16) for s in range(2)]
        wv = RW.alloc("wv", [8, 256], BF16)
        t1 = [RW.alloc(("ct1", s), [512], F32) for s in range(2)]
        t2 = [RW.alloc(("ct2", s), [512], F32) for s in range(2)]
        agK = RW.alloc("agK", [4, 128], F32)
        agV = RW.alloc("agV", [4, 64], F32)
        self.c0_end = RW.off
        RW.alias_keys(self.keys_of)
        kf2, Vall = self.kf2, self.Vall
        self.dma("sp", self.cosT, self.cos_d, [], ["cosT"], "cosT")
        self.dma("sp", self.sinT, self.sin_d, [], ["sinT"], "sinT")
        self.memset(Vall.rearrange("p a b c -> p (a b c)"), 1.0, [("Vall", b) for b in range(NB + 1)], eng="pool")
        AKC = 4096 + 1024
        AVC = AKC + 256
        for g in range(4):
            s = g % 2
            self.load_w_cast(wk2[s][:, :, 0:64], self.w_in[l], AKC + g * 64, 64, ("wk2", s), f"wk2a{s}", 8)
            self.load_w_cast(wk2[s][:, :, 64:128], self.w_in[l], AKC + g * 64, 64, ("wk2", s), f"wk2b{s}", 8)
            self.rot_weights(wkr2[s], wk2[s], ("wkr2", s), ("wk2", s), 2)
            for i in range(NT):
                t0 = i * 512
                hk = [("hT", i * 4 + j) for j in range(4)]
                q = (g * NT + i) % 2
                pk, pkr = ps[0 + q], ps[2 + q]
                for kc in range(8):
                    self.mm(pk, wk2[s][:, kc, :], hT[:, kc, t0:t0 + 512], kc == 0, kc == 7,
                            hk + [("wk2", s)], [("ps", 0 + q)])
                for kc in range(8):
                    self.mm(pkr, wkr2[s][:, kc, :], hT[:, kc, t0:t0 + 512], kc == 0, kc == 7,
                            hk + [("wkr2", s)], [("ps", 2 + q)])
                self.tt(t1[q], pk, self.cosT[:, t0:t0 + 512], ALU.mult, [("ps", 0 + q), "cosT"], [("ct1", q)])
                self.tt(t2[q], pkr, self.sinT[:, t0:t0 + 512], ALU.mult, [("ps", 2 + q), "sinT"], [("ct2", q)])
                self.tt(kf2[:, g, 128 + t0:128 + t0 + 512], t1[q], t2[q], ALU.add, [("ct1", q), ("ct2", q)],
                        [("kf2", g, 1 + i * 4 + j) for j in range(4)])
            if write_ag:
                self.cpy(agK[:, g, :], kf2[:, g, T:T + 128], [("kf2", g, NB)], [("agK", g)])
        self.load_w_cast(wv, self.w_in[l], AVC, 256, "wv", "wv", 8)
        for b in range(NB):
            q = b % 2
            for kc in range(8):
                self.mm(ps[4 + q][:, 0:256], hT[:, kc, b * 128:(b + 1) * 128], wv[:, kc, :], kc == 0, kc == 7,
                        [("hT", b), "wv"], [("ps", 4 + q)])
            self.act(Vall[:, 1 + b, :, 0:64], ps[4 + q][:, 0:256].rearrange("p (a b) -> p a b", a=4), AF.Copy,
                     [("ps", 4 + q)], [("Vall", 1 + b)])
        if write_ag:
            self.cpy(agV, Vall[:, NB, :, 0:64], [("Vall", NB)], ["agV"])
            self.out_ops.append(self.dma("sp", self.agin[:, 1032:1544], agK.rearrange("p a b -> p (a b)"),
                                         [("agK", g) for g in range(4)], ["agin_K"], "aginK").id)
            self.out_ops.append(self.dma("sp", self.agin[:, 1544:1800], agV.rearrange("p a b -> p (a b)"), ["agV"],
                                         ["agin_V"], "aginV").id)

    def rot_weights(self, wr, w, kwr, kw, nheads):
        self.memset(wr.rearrange("p a b -> p (a b)"), 0.0, [kwr], eng="pool")
        for hh in range(nheads):
            o = hh * 64
            self.ts(wr[:, :, o:o + 8], w[:, :, o + 8:o + 16], -1.0, None, ALU.mult, ALU.bypass, [kw, kwr], [kwr],
                    eng="pool")
            self.cpy(wr[:, :, o + 8:o + 16], w[:, :, o:o + 8], [kw, kwr], [kwr], eng="pool")

    def phase_ag(self, l):
        P, RW = self.P, self.RW
        NB = self.NB
        ag_reads = ["agin_S", "agin_D", "agin_K", "agin_V"]
        agin, agout = self.agin, self.agout

        gath1 = RW.alloc("gath", [AGC], F32)
        acc = RW.alloc("sacc", [1024], F32)
        tmp = RW.alloc("stmp", [1024], F32)
        kacc = RW.alloc("kacc", [512], F32)
        vacc = RW.alloc("vacc", [256], F32)
        self.c1_start = RW.off
        RW.alias_keys(self.keys_of)
        sel = self.sel
        self.memset(acc, 0.0, ["sacc"])
        self.memset(kacc, 0.0, ["kacc"])
        self.memset(vacc, 0.0, ["vacc"])
        acc3 = acc.rearrange("p (h v) -> p h v", h=NH)
        tmp3 = tmp.rearrange("p (h v) -> p h v", h=NH)
        for j in range(3):
            self.dma("sp", gath1, agout[j * 128:(j + 1) * 128, :], ["agout"], ["gath"], "gath")
            Sj = gath1[:, 0:1024]
            Dj = gath1[:, 1024:1032]
            self.tt(tmp3, acc3, Dj.unsqueeze(2).to_broadcast([128, NH, 128]), ALU.mult, ["sacc", "gath"], ["stmp"])
            self.tt(tmp, tmp, Sj, ALU.add, ["stmp", "gath"], ["stmp"])
            self.tt(tmp, tmp, acc, ALU.subtract, ["stmp", "sacc"], ["stmp"])
            self.stt(acc, tmp, sel[:, j:j + 1], acc, ALU.mult, ALU.add, ["stmp", "sel", "sacc"], ["sacc"])
            for (accx, c0, n, key) in ((kacc, 1032, 512, "kacc"), (vacc, 1544, 256, "vacc")):
                self.stt(accx, gath1[:, c0:c0 + n], sel[:, 3 + j:4 + j], accx, ALU.mult, ALU.add,
                         ["gath", "sel", key], [key])
        self.cpy(self.Sin_bf.rearrange("p a b -> p (a b)"), acc, ["sacc"], ["Sin_bf"])
        if l == 0:
            self.dump("Sin", self.Sin_bf, ["Sin_bf"])
        self.cpy(self.kf2[:, :, 0:128], kacc.rearrange("p (a b) -> p a b", a=4), ["kacc"],
                 [("kf2", g, 0) for g in range(4)])
        self.cpy(self.Vall[:, 0, :, 0:64], vacc.rearrange("p (a b) -> p a b", a=4), ["vacc"], [("Vall", 0)])

    def phase_c1(self, l):
        P, RW = self.P, self.RW
        T, NT, NB = self.T, self.NT, self.NB
        hT, ps = self.hT, self.ps
        kf2, Vall = self.kf2, self.Vall
        RW.truncate(self.c_mark)
        wq4 = [RW.alloc(("wq4", s), [8, 256], BF16) for s in range(2)]
        wqr = [RW.alloc(("wqr", s), [8, 256], BF16) for s in range(2)]
        t1 = [RW.alloc(("qt1", s), [512], F32) for s in range(2)]
        t2 = [RW.alloc(("qt2", s), [512], F32) for s in range(2)]
        qf = [[RW.alloc(("qf", s, pr), [512], BF16) for pr in range(2)] for s in range(2)]
        pT = [[RW.alloc(("pT", s, par), [512], BF16) for par in range(2)] for s in range(2)]
        den = [RW.alloc(("den", s), [4], F32) for s in range(2)]
        rec = [RW.alloc(("rec", s), [4], F32) for s in range(2)]
        att = [RW.alloc(("att", s), [4, 64], BF16) for s in range(2)]
        attT = [RW.alloc(("attT", s), [2, 512], BF16) for s in range(2)]
        for s in range(2):
            self.reg(("attT", s), [("attT", s, j) for j in range(4)])
        RW.alias_keys(self.keys_of)
        self.load_bcast(self.esink, self.attn_sinks[l:l + 1, :], "esink", "esink")
        self.act(self.esink, self.esink, AF.Exp, ["esink"], ["esink"])
        AQC = 4096
        ident = self.ident
        blk = 0
        for g in range(4):
            s = g % 2
            self.load_w_cast(wq4[s], self.w_in[l], AQC + g * 256, 256, ("wq4", s), f"wq4{s}", 8)
            self.rot_weights(wqr[s], wq4[s], ("wqr", s), ("wq4", s), 4)
            for i in range(NT):
                t0 = i * 512
                ti = (g * NT + i) % 2
                hk = [("hT", i * 4 + j) for j in range(4)]
                for pr in range(2):
                    q = pr
                    for kc in range(8):
                        self.mm(ps[0 + q], wq4[s][:, kc, pr * 128:(pr + 1) * 128], hT[:, kc, t0:t0 + 512],
                                kc == 0, kc == 7, hk + [("wq4", s)], [("ps", 0 + q)])
                    for kc in range(8):
                        self.mm(ps[2 + q], wqr[s][:, kc, pr * 128:(pr + 1) * 128], hT[:, kc, t0:t0 + 512],
                                kc == 0, kc == 7, hk + [("wqr", s)], [("ps", 2 + q)])
                    self.tt(t1[q], ps[0 + q], self.cosT[:, t0:t0 + 512], ALU.mult, [("ps", 0 + q), "cosT"], [("qt1", q)])
                    self.tt(t2[q], ps[2 + q], self.sinT[:, t0:t0 + 512], ALU.mult, [("ps", 2 + q), "sinT"], [("qt2", q)])
                    self.tt(qf[ti][pr], t1[q], t2[q], ALU.add, [("qt1", q), ("qt2", q)], [("qf", ti, pr)])
                for bi in range(4):
                    b = i * 4 + bi
                    bs = blk % 2
                    blk += 1
                    mb = self.maskb0 if b == 0 else self.maskb
                    mbk = "maskb0" if b == 0 else "maskb"
                    for par in range(2):
                        pS = ps[4 + par]
                        self.mm(pS, ident, mb, True, False, ["ident", mbk], [("ps", 4 + par)])
                        for pr in range(2):
                            for which in range(2):
                                kb_ = b + 1 - which
                                last = (pr == 1 and which == 1)
                                self.mm(pS[:, (pr * 2 + which) * 128:(pr * 2 + which + 1) * 128],
                                        kf2[par * 64:(par + 1) * 64, g, kb_ * 128:(kb_ + 1) * 128],
                                        qf[ti][pr][par * 64:(par + 1) * 64, bi * 128:(bi + 1) * 128],
                                        False, last, [("kf2", g, kb_), ("qf", ti, pr)], [("ps", 4 + par)])
                        self.act(pT[bs][par], pS, AF.Exp, [("ps", 4 + par)], [("pT", bs, par)], scale=0.125)
                    pO = ps[6]
                    for u in range(4):
                        pr, par = u // 2, u % 2
                        for which in range(2):
                            vb = b + 1 - which
                            self.mm(pO[:, u * 65:(u + 1) * 65],
                                    pT[bs][par][:, (pr * 2 + which) * 128:(pr * 2 + which + 1) * 128],
                                    Vall[:, vb, g, :], which == 0, which == 1,
                                    [("pT", bs, par), ("Vall", vb)], [("ps", 6)])
                    pO3 = pO[:, 0:260].rearrange("p (a b) -> p a b", a=4)
                    self.tt(den[bs], pO3[:, :, 64], self.esink[:, g * 4:(g + 1) * 4], ALU.add, [("ps", 6), "esink"],
                            [("den", bs)])
                    self.recip(rec[bs], den[bs], [("den", bs)], [("rec", bs)])
                    self.tt(att[bs], pO3[:, :, 0:64], rec[bs].unsqueeze(2).to_broadcast([128, 4, 64]), ALU.mult,
                            [("ps", 6), ("rec", bs)], [("att", bs)])
                    pTr = ps[7].bitcast(BF16)[:, 0:256].rearrange("p (a b) -> p a b", a=2)
                    attf = att[bs].rearrange("p a b -> p (a b)")
                    for pr in range(2):
                        self.tr(pTr[:, pr, :], attf[:, pr * 128:(pr + 1) * 128], [("att", bs)], [("ps", 7)])
                    self.act(attT[ti][:, :, bi * 128:(bi + 1) * 128], pTr, AF.Copy, [("ps", 7)], [("attT", ti, bi)])
                for pr in range(2):
                    self.dma("sp", self.attTd[g * 2 + pr, :, t0:t0 + 512], attT[ti][:, pr, :],
                             [("attT", ti, j) for j in range(4)], [("attTd", g * 2 + pr, i)], f"attT{ti}{pr}")

    def phase_d1(self, l):
        P, RW = self.P, self.RW
        T, NT = self.T, self.NT
        hT, ps = self.hT, self.ps
        RW.reset()
        whg = RW.alloc("whg", [8, 1024], BF16)
        hgn = RW.alloc("hgn", [NH], F32)
        ol = [RW.alloc(("d_ol", s), [512], F32) for s in range(2)]
        qg = [RW.alloc(("d_qg", s), [512], BF16) for s in range(2)]
        o = [RW.alloc(("d_o", s), [512], F32) for s in range(2)]
        sq = [RW.alloc(("d_sq", s), [512], BF16) for s in range(2)]
        r1 = [RW.alloc(("d_r1", s), [512], F32) for s in range(2)]
        sgt = [RW.alloc(("d_sg", s), [512], F32) for s in range(2)]
        og = [RW.alloc(("d_og", s), [512], BF16) for s in range(2)]
        RW.alias_keys(self.keys_of)
        self.load_w_cast(whg, self.w_in[l], 3072, 1024, "whg", "whg", 8)
        hgd = self.hg_norm

        def fn_hg(e_, sm):
            for h_ in range(NH):
                e_.dma_start(out=hgn[:, h_:h_ + 1],
                             in_=hgd[l:l + 1, h_ * 128:(h_ + 1) * 128].rearrange("o k -> k o")).then_inc(sm, 16)
        self.P.add("sp", fn_hg, [], ["hgn"], semkey="hgn", ndma=NH)
        it = 0
        for i in range(NT):
            t0 = i * 512
            hk = [("hT", i * 4 + j) for j in range(4)]
            for h in range(NH):
                s = it % 2
                it += 1
                K = lambda n: (n, s)
                self.dma("sp", ol[s], self.ohl[h, :, t0:t0 + 512], [("ohl", h, i)], [K("d_ol")], f"dol{s}")
                self.dma("sp", qg[s], self.qgd[h, :, t0:t0 + 512], [("qgd", h, i)], [K("d_qg")], f"dqg{s}")
                pc, pss, ph = ps[0 + s], ps[2 + s], ps[4 + s]
                self.mm(pc, self.Sin_bf[:, h, :], qg[s], True, True, ["Sin_bf", K("d_qg")], [("ps", 0 + s)])
                for kc in range(8):
                    self.mm(ph, whg[:, kc, h * 128:(h + 1) * 128], hT[:, kc, t0:t0 + 512], kc == 0, kc == 7,
                            hk + ["whg"], [("ps", 4 + s)])
                self.tt(o[s], pc, ol[s], ALU.add, [("ps", 0 + s), K("d_ol")], [K("d_o")])
                self.act(sq[s], o[s], AF.Square, [K("d_o")], [K("d_sq")])
                self.mm(pss, self.ones, sq[s], True, True, ["ones", K("d_sq")], [("ps", 2 + s)])
                self.ts(r1[s], pss, 1.0 / 128, EPS, ALU.mult, ALU.add, [("ps", 2 + s)], [K("d_r1")])
                self.act(r1[s], r1[s], AF.Sqrt, [K("d_r1")], [K("d_r1")])
                self.recip(r1[s], r1[s], [K("d_r1")], [K("d_r1")])
                self.act(sgt[s], ph, AF.Silu, [("ps", 4 + s)], [K("d_sg")])
                self.stt(o[s], o[s], hgn[:, h:h + 1], r1[s], ALU.mult, ALU.mult, [K("d_o"), "hgn", K("d_r1")], [K("d_o")])
                self.tt(og[s], o[s], sgt[s], ALU.mult, [K("d_o"), K("d_sg")], [K("d_og")])
                self.dma("sp", self.ohgd[h, :, t0:t0 + 512], og[s], [K("d_og")], [("ohgd", h, i)], f"dog{s}")

    def phase_d2(self, l, xin, xout):
        P, RW = self.P, self.RW
        T, NT = self.T, self.NT
        hT, ps = self.hT, self.ps
        RW.reset()
        wpa = RW.alloc("wpa", [8, 1024], BF16)
        wpb = RW.alloc("wpb", [8, 1024], BF16)
        wo = RW.alloc("wo", [8, 1024], BF16)
        wga = RW.alloc("wga", [8, 1024], BF16)
        wgb = RW.alloc("wgb", [8, 1024], BF16)
        ohg = [RW.alloc(("e_ohg", s), [8, 512], BF16) for s in range(2)]
        atT = [RW.alloc(("e_att", s), [8, 512], BF16) for s in range(2)]
        mixT = RW.alloc("mixT", [8, 512], BF16)
        self.reg("mixT", [("mixT", m) for m in range(8)])
        sga = [RW.alloc(("e_sga", 0), [512], F32)] * 2
        sgb = [RW.alloc(("e_sgb", 0), [512], F32)] * 2
        xr = [RW.alloc(("e_xr", s), [D], F32) for s in range(2)]
        RW.alias_keys(self.keys_of)
        self.load_w_cast(wpa, self.w_pa[l], 0, 1024, "wpa", "wpa", 8)
        self.load_w_cast(wpb, self.w_pb[l], 0, 1024, "wpb", "wpb", 8)
        self.load_w_cast(wga, self.w_in[l], 5632, 1024, "wga", "wga", 8)
        self.load_w_cast(wgb, self.w_in[l], 6656, 1024, "wgb", "wgb", 8)
        self.load_w_cast(wo, self.w_o[l], 0, 1024, "wo", "wo", 8)
        cnt = 0
        for i in range(NT):
            t0 = i * 512
            s = i % 2
            hk = [("hT", i * 4 + j) for j in range(4)]
            self.dma("sp", ohg[s], self.ohgd[:, :, t0:t0 + 512].rearrange("h p t -> p h t"),
                     [("ohgd", h, i) for h in range(NH)], [("e_ohg", s)], f"eohg{s}")
            self.dma("sp", atT[s], self.attTd[:, :, t0:t0 + 512].rearrange("h p t -> p h t"),
                     [("attTd", kc, i) for kc in range(8)], [("e_att", s)], f"eatt{s}")
            for m in range(8):
                q = m % 2
                pya, pyb, pga, pgb = ps[0 + q], ps[2 + q], ps[4 + q], ps[6 + q]
                msl = slice(m * 128, (m + 1) * 128)
                for kc in range(8):
                    self.mm(pya, wpa[:, kc, msl], ohg[s][:, kc, :], kc == 0, kc == 7, ["wpa", ("e_ohg", s)], [("ps", 0 + q)])
                for kc in range(8):
                    self.mm(pyb, wpb[:, kc, msl], atT[s][:, kc, :], kc == 0, kc == 7, ["wpb", ("e_att", s)], [("ps", 2 + q)])
                for kc in range(8):
                    self.mm(pga, wga[:, kc, msl], hT[:, kc, t0:t0 + 512], kc == 0, kc == 7, ["wga"] + hk, [("ps", 4 + q)])
                for kc in range(8):
                    self.mm(pgb, wgb[:, kc, msl], hT[:, kc, t0:t0 + 512], kc == 0, kc == 7, ["wgb"] + hk, [("ps", 6 + q)])
                self.act(sga[q], pga, AF.Sigmoid, [("ps", 4 + q)], [("e_sga", 0)])
                self.act(sgb[q], pgb, AF.Sigmoid, [("ps", 6 + q)], [("e_sgb", 0)])
                self.tt(sga[q], sga[q], pya, ALU.mult, [("e_sga", 0), ("ps", 0 + q)], [("e_sga", 0)])
                self.tt(sgb[q], sgb[q], pyb, ALU.mult, [("e_sgb", 0), ("ps", 2 + q)], [("e_sgb", 0)])
                self.tt(mixT[:, m, :], sga[q], sgb[q], ALU.add, [("e_sga", 0), ("e_sgb", 0)], [("mixT", m)])
            for tb in range(4):
                r0 = t0 + tb * 128
                xs = tb % 2
                self.dma("sp", xr[xs], xin[r0:r0 + 128, :], [("dram", xin.tensor.name, r0)], [("e_xr", xs)], f"exr{xs}")
                for ch in range(2):
                    q = cnt % 2
                    cnt += 1
                    pd = ps[0 + q]
                    for m in range(8):
                        self.mm(pd, mixT[:, m, tb * 128:(tb + 1) * 128], wo[:, m, ch * 512:(ch + 1) * 512],
                                m == 0, m == 7, [("mixT", m), "wo"], [("ps", 0 + q)])
                    xsl = xr[xs][:, ch * 512:(ch + 1) * 512]
                    self.tt(xsl, pd, xsl, ALU.add, [("ps", 0 + q), ("e_xr", xs)], [("e_xr", xs)])
                self.dma("sp", xout[r0:r0 + 128, :], xr[xs], [("e_xr", xs)], [("dram", xout.tensor.name, r0)], f"exr{xs}")

    def ffn_phase(self, l, xin, xout):
        P, RW, RH = self.P, self.RW, self.RH
        NT = self.NT
        RW.reset()
        RH.reset()
        wg = RW.alloc("wg", [8, FF], BF16)
        wu = RW.alloc("wu", [8, FF], BF16)
        wd = RW.alloc("wd", [NJ, D], BF16)
        g2 = self.g
        xblk = [RH.alloc(("xblk", s), [D], F32) for s in range(2)]
        xr = [RH.alloc(("xr", s), [D], F32) for s in range(2)]
        hn = [RH.alloc(("hn", s), [D], BF16) for s in range(2)]
        junk = RH.alloc("junk", [D], BF16)
        h2T = [RH.alloc(("h2T", s), [8, 512], BF16) for s in range(2)]
        actT = RH.alloc("actT", [NJ, 512], BF16)
        sgt = [RH.alloc(("sgt", s), [512], F32) for s in range(2)]
        self.reg("actT", [("actT", j) for j in range(NJ)])
        for s in range(2):
            self.reg(("h2T", s), [("h2T", s, j) for j in range(4)])
        RW.alias_keys(self.keys_of)
        RH.alias_keys(self.keys_of)
        self.load_bcast(g2, self.norm2[l:l + 1, :], "g", "g")
        self.load_w_cast(wg, self.w_gate[l], 0, FF, "wg", "wg", 8)
        self.load_w_cast(wu, self.w_up[l], 0, FF, "wu", "wu", 8)
        self.load_w_cast(wd, self.w_down[l], 0, D, "wd", "wd", NJ)
        ps = self.ps
        cnt = {"blk": 0, "gu": 0, "dn": 0}

        def norm_block(i, j):
            b = cnt["blk"]
            cnt["blk"] += 1
            s = b % 2
            r0 = i * 512 + j * 128
            self.dma("sp", xblk[s], xin[r0:r0 + 128, :], [("dram", xin.tensor.name, r0)], [("xblk", s)], f"xblk{s}")
            self.rms_block(xblk[s], ("xblk", s), s, hn[s], ("hn", s), g2, "g", junk, "junk")
            pstv = ps[s].bitcast(BF16).rearrange("p (k t) -> p k t", k=8)
            for kc in range(8):
                self.tr(pstv[:, kc, :], hn[s][:, kc * 128:(kc + 1) * 128], [("hn", s)], [("ps", s)])
            self.act(h2T[i % 2][:, :, j * 128:(j + 1) * 128], pstv, AF.Copy, [("ps", s)], [("h2T", i % 2, j)])

        def gate_up(i, j):
            gq = cnt["gu"]
            cnt["gu"] += 1
            s = gq % 2
            hs = i % 2
            hk = [("h2T", hs, jj) for jj in range(4)]
            for kc in range(8):
                self.mm(ps[2 + s], wg[:, kc, j * 128:(j + 1) * 128], h2T[hs][:, kc, :], kc == 0, kc == 7,
                        hk + ["wg"], [("ps", 2 + s)])
            for kc in range(8):
                self.mm(ps[4 + s], wu[:, kc, j * 128:(j + 1) * 128], h2T[hs][:, kc, :], kc == 0, kc == 7,
                        hk + ["wu"], [("ps", 4 + s)])
            self.act(sgt[s], ps[2 + s], AF.Silu, [("ps", 2 + s)], [("sgt", s)])
            self.tt(actT[:, j, :], sgt[s], ps[4 + s], ALU.mult, [("sgt", s), ("ps", 4 + s)], [("actT", j)])

        def down(i, tb):
            r0 = i * 512 + tb * 128
            s = tb % 2
            self.dma("sp", xr[s], xin[r0:r0 + 128, :], [("dram", xin.tensor.name, r0)], [("xr", s)], f"xr{s}")
            for ch in range(2):
                d = cnt["dn"]
                cnt["dn"] += 1
                pd = ps[6 + d % 2]
                pk = ("ps", 6 + d % 2)
                for j in range(NJ):
                    self.mm(pd, actT[:, j, tb * 128:(tb + 1) * 128], wd[:, j, ch * 512:(ch + 1) * 512],
                            j == 0, j == NJ - 1, [("actT", j), "wd"], [pk])
                xs = xr[s][:, ch * 512:(ch + 1) * 512]
                self.tt(xs, pd, xs, ALU.add, [pk, ("xr", s)], [("xr", s)])
            self.out_ops.append(self.dma("sp", xout[r0:r0 + 128, :], xr[s], [("xr", s)],
                                         [("dram", xout.tensor.name, r0)], f"xr{s}").id)

        for j in range(4):
            norm_block(0, j)
        for i in range(NT):
            for j in range(NJ):
                gate_up(i, j)
                if i + 1 < NT and j in (3, 8, 13, 18):
                    norm_block(i + 1, (j - 3) // 5)
            for tb in range(4):
                down(i, tb)

    def final_phase(self, xin):
        P, RW, RH = self.P, self.RW, self.RH
        RW.reset()
        RH.reset()
        gf = self.g
        xblk = [RW.alloc(("fx", s), [D], F32) for s in range(4)]
        yo = [RW.alloc(("fy", s), [D], F32) for s in range(4)]
        junk = RW.alloc("fjunk", [D], BF16)
        RW.alias_keys(self.keys_of)
        RH.alias_keys(self.keys_of)
        self.load_bcast(gf, self.final_norm[0:1, :], "g", "g")
        finals = []
        for b in range(self.T // 128):
            s = b % 4
            r0 = b * 128
            self.dma("sp", xblk[s], xin[r0:r0 + 128, :], [("dram", xin.tensor.name, r0)], [("fx", s)], f"fx{s}")
            self.rms_block(xblk[s], ("fx", s), s, yo[s], ("fy", s), gf, "g", junk, "fjunk")
            op = self.dma("sp", self.y[r0:r0 + 128, :], yo[s], [("fy", s)], [("y", r0)], f"fy{s}")
            finals.append(op.id)
        return finals


_CACHE = {}


def _host_consts(T):
    ident = np.eye(128, dtype=np.float32)
    p = np.arange(128)[:, None]
    t64 = np.arange(64)[None, :]
    maskneg = np.where((p % 64) <= t64, -1.0, 0.0).astype(np.float32)
    t128 = np.arange(128)[None, :]
    cur = np.where(p <= t128, 0.0, NEG).astype(np.float32)
    prev = np.where(p > t128, 0.0, NEG).astype(np.float32)
    maskb = np.concatenate([cur, prev, cur, prev], axis=1)
    cbf = np.concatenate([ident, maskneg, maskb], axis=1).astype(ml_dtypes.bfloat16)
    half = 8
    inv = (np.float32(500000.0) ** (-np.arange(half, dtype=np.float32) * np.float32(2.0) / np.float32(16))).astype(np.float32)
    per_core = []
    for c in range(NCORES):
        sg = c % SEG
        pos = (sg * T + np.arange(T)).astype(np.float32)
        ang = (pos[:, None] * inv[None, :]).astype(np.float32)
        cosv = np.cos(ang.astype(np.float64)).astype(np.float32)
        sinv = np.sin(ang.astype(np.float64)).astype(np.float32)
        cosT = np.ones((128, T), np.float32)
        sinT = np.zeros((128, T), np.float32)
        for hh in range(2):
            o = hh * 64
            cosT[o:o + 8] = cosv.T
            cosT[o + 8:o + 16] = cosv.T
            sinT[o:o + 8] = sinv.T
            sinT[o + 8:o + 16] = sinv.T
        sel = np.zeros((128, 8), np.float32)
        for j in range(3):
            sel[:, j] = 1.0 if j < sg else 0.0
            sel[:, 3 + j] = 1.0 if j == sg - 1 else 0.0
        sel[:, 7] = NEG if sg == 0 else 0.0
        per_core.append({"rope_cos": cosT, "rope_sin": sinT, "sel": sel})
    return cbf, per_core


def _get_prog(T, mode, final, debug=False):
    key = (T, mode, final, debug)
    if key not in _CACHE:
        bld = Builder(T, mode=mode, final=final, debug=debug)
        nc = bld.build()
        _CACHE[key] = (bld, nc)
    return _CACHE[key]


def kernel(x, norm1, w_in, lb_logits, hg_norm, attn_sinks, w_pa, w_pb, w_o, norm2, w_gate, w_up, w_down,
           final_norm, _T=None, _L=None, _debug=False):
    x = np.asarray(x, dtype=np.float32)
    B, S, _ = x.shape
    T = S // SEG if _T is None else _T
    L = np.asarray(norm1).shape[0] if _L is None else _L
    f = lambda a: np.ascontiguousarray(np.asarray(a, dtype=np.float32))
    cbf, per_core = _host_consts(T)
    lb4 = np.zeros((4, D), np.float32)
    lbl = f(lb_logits)
    lb4[:lbl.shape[0]] = lbl
    norm1, w_in, hg_norm, attn_sinks = f(norm1), f(w_in), f(hg_norm), f(attn_sinks)
    w_pa, w_pb, w_o, norm2 = f(w_pa), f(w_pb), f(w_o), f(norm2)
    w_gate, w_up, w_down = f(w_gate), f(w_up), f(w_down)
    fn = f(final_norm).reshape(1, D)
    xcur = [np.ascontiguousarray(x[c // SEG, (c % SEG) * T:(c % SEG + 1) * T, :]) for c in range(NCORES)]
    kernel.dbg = []
    for l in range(L):
        sl = slice(l, l + 1)
        lsel = np.zeros((128, 4), np.float32)
        lsel[:, l] = 1.0
        _, nc1 = _get_prog(T, "p1", False, _debug)
        in1 = []
        for c in range(NCORES):
            m = {"x": xcur[c], "norm1": norm1[sl], "w_in": w_in[sl], "lb_logits": lb4, "lsel": lsel, "cbf": cbf}
            m.update(per_core[c])
            in1.append(m)
        r1 = run_bass_kernel_spmd(nc1, in1, core_ids=list(range(NCORES))).results
        agout = [np.concatenate([r1[(c // SEG) * SEG + j]["agin"] for j in range(SEG)], axis=0) for c in range(NCORES)]
        bld2, nc2 = _get_prog(T, "p2", l == L - 1, _debug)
        in2 = []
        for c in range(NCORES):
            m = {"x": xcur[c], "norm1": norm1[sl], "w_in": w_in[sl], "hg_norm": hg_norm[sl],
                 "attn_sinks": attn_sinks[sl], "w_pa": w_pa[sl], "w_pb": w_pb[sl], "w_o": w_o[sl],
                 "norm2": norm2[sl], "w_gate": w_gate[sl], "w_up": w_up[sl], "w_down": w_down[sl],
                 "final_norm": fn, "ohl": r1[c]["ohl"], "qgd": r1[c]["qgd"], "agout": agout[c], "cbf": cbf}
            m.update(per_core[c])
            in2.append(m)
        r2 = run_bass_kernel_spmd(nc2, in2, core_ids=list(range(NCORES))).results
        xcur = [r2[c]["y"] for c in range(NCORES)]
        if _debug:
            kernel.dbg.append(([{k: v for k, v in r1[c].items()} for c in range(NCORES)],
                               [{k: v for k, v in r2[c].items()} for c in range(NCORES)]))
    out = np.empty((B, SEG * T, D), dtype=np.float32)
    for c in range(NCORES):
        out[c // SEG, (c % SEG) * T:(c % SEG + 1) * T, :] = xcur[c]
    return out
```
